# Optimizing a Trainium2 kernel written in Bass

```python
import jax, jax.numpy as jnp
from jax import lax
import numpy as np

D_MODEL = 2048
BATCH = 4
SEQ = 4096
DEPTH = 2

RET_HEADS = 4
RET_HEAD_DIM = 256
RET_WIDTH = RET_HEADS * RET_HEAD_DIM
RET_CHUNK = 128
RET_ROPE_BASE = 10000.0
MOBA_HEADS = 8
MOBA_HEAD_DIM = 128
MOBA_WIDTH = MOBA_HEADS * MOBA_HEAD_DIM
MOBA_BLOCK = 256
MOBA_TOPK = 3
MOBA_QCHUNK = 64
ROPE_THETA = 500000.0
ROPE_DIM = MOBA_HEAD_DIM // 4
HGRN_HEADS = 8
HGRN_HEAD_DIM = 128
HGRN_WIDTH = HGRN_HEADS * HGRN_HEAD_DIM
HGRN_CHUNK = 64
FFN_HIDDEN = ((8 * D_MODEL + 3 * 256 - 1) // (3 * 256)) * 256
NORM_EPS = 1e-6
IN_SIZES = (RET_WIDTH,) * 4 + (MOBA_WIDTH,) * 3 + (HGRN_WIDTH,) * 4 + (D_MODEL,) * 3
IN_COLS = sum(IN_SIZES)
IN_SPLITS = tuple(sum(IN_SIZES[:i + 1]) for i in range(len(IN_SIZES) - 1))

kernel_name = "hybrid_retention_moba_hgrn2_block"


def rmsnorm(x, g):
    xf = x.astype(jnp.float32)
    y = xf * lax.rsqrt(jnp.mean(xf * xf, axis=-1, keepdims=True) + NORM_EPS)
    return (y * g.astype(jnp.float32)).astype(x.dtype)


def head_layer_norm(o):
    mu = jnp.mean(o, axis=-1, keepdims=True)
    c = o - mu
    return c * lax.rsqrt(jnp.mean(c * c, axis=-1, keepdims=True) + NORM_EPS)


def head_rms_norm(o):
    return o * lax.rsqrt(jnp.mean(o * o, axis=-1, keepdims=True) + NORM_EPS)


def split_heads(t, n_heads):
    B, S, W = t.shape
    return t.reshape(B, S, n_heads, W // n_heads).transpose(0, 2, 1, 3)


def merge_heads(t):
    B, H, S, d = t.shape
    return t.transpose(0, 2, 1, 3).reshape(B, S, H * d)


def rotary(x, positions, rot_dim, base):
    half = rot_dim // 2
    inv_freq = base ** (-jnp.arange(half, dtype=jnp.float32) * 2.0 / rot_dim)
    ang = positions.astype(jnp.float32)[:, None] * inv_freq[None, :]
    cos = jnp.cos(ang).astype(x.dtype)
    sin = jnp.sin(ang).astype(x.dtype)
    x1 = x[..., :half]
    x2 = x[..., half:rot_dim]
    return jnp.concatenate([x1 * cos - x2 * sin, x2 * cos + x1 * sin, x[..., rot_dim:]], axis=-1)


def retention(q, k, v):
    B, H, S, dk = q.shape
    C = RET_CHUNK
    NC = S // C
    log_gamma = jnp.log1p(-jnp.exp2(-5.0 - jnp.arange(H, dtype=jnp.float32)))
    k = k * (dk ** -0.5)
    qc = q.reshape(B, H, NC, C, dk)
    kc = k.reshape(B, H, NC, C, dk)
    vc = v.reshape(B, H, NC, C, -1)
    t = jnp.arange(C, dtype=jnp.float32)
    rel = t[:, None] - t[None, :]
    decay = jnp.where(rel >= 0, jnp.exp(log_gamma[:, None, None] * jnp.maximum(rel, 0.0)), 0.0)
    scores = jnp.einsum('bhnid,bhnjd->bhnij', qc, kc) * decay[None, :, None]
    intra = jnp.einsum('bhnij,bhnje->bhnie', scores, vc)
    k_w = kc * jnp.exp(log_gamma[:, None] * (C - 1.0 - t))[None, :, None, :, None]
    kv = jnp.einsum('bhncd,bhnce->nbhde', k_w, vc)
    chunk_decay = jnp.exp(log_gamma * C)[None, :, None, None]

    def step(state, kv_n):
        return chunk_decay * state + kv_n, state

    _, r_prev = lax.scan(step, jnp.zeros_like(kv[0]), kv)
    q_w = qc * jnp.exp(log_gamma[:, None] * (t + 1.0))[None, :, None, :, None]
    cross = jnp.einsum('bhncd,nbhde->bhnce', q_w, r_prev)
    return (intra + cross).reshape(B, H, S, -1)


def moba_attention(q, k, v):
    B, H, S, d = q.shape
    L = MOBA_BLOCK
    QC = MOBA_QCHUNK
    NQ = S // QC
    pad = (-S) % L
    k_p = jnp.pad(k, ((0, 0), (0, 0), (0, pad), (0, 0)))
    v_p = jnp.pad(v, ((0, 0), (0, 0), (0, pad), (0, 0)))
    NB = (S + pad) // L
    K = min(MOBA_TOPK, NB)
    scale = d ** -0.5
    kb = k_p.reshape(B, H, NB, L, d)
    vb = v_p.reshape(B, H, NB, L, d)
    k_mean = jnp.mean(kb.astype(jnp.float32), axis=3)
    q_block = jnp.arange(S) // L
    gate = jnp.einsum('bhsd,bhnd->bhsn', q.astype(jnp.float32), k_mean)
    fully_past = jnp.arange(NB)[None, :] < q_block[:, None]
    gate = jnp.where(fully_past, gate, -jnp.inf)
    _, sel = lax.top_k(gate, K)
    valid = sel < q_block[:, None]

    def to_qchunks(t):
        t = t.reshape((B, H, NQ, QC) + t.shape[3:])
        t = jnp.swapaxes(t, 1, 2)
        return t.reshape((B * NQ, H, QC) + t.shape[4:])

    b_idx = jnp.repeat(jnp.arange(B), NQ)
    c_idx = jnp.tile(jnp.arange(NQ), B)
    h_idx = jnp.arange(H)[:, None, None]

    def attend(xs):
        qn, seln, validn, bi, ci = xs
        kbb = kb[bi]
        vbb = vb[bi]
        k_sel = kbb[h_idx, seln]
        v_sel = vbb[h_idx, seln]
        s_sel = jnp.einsum('hqd,hqkld->hqkl', qn, k_sel).astype(jnp.float32) * scale
        s_sel = jnp.where(validn[..., None], s_sel, -jnp.inf).reshape(H, QC, K * L)
        j = (ci * QC) // L
        k_own = kbb[:, j]
        v_own = vbb[:, j]
        q_pos = ci * QC + jnp.arange(QC)
        key_pos = j * L + jnp.arange(L)
        s_own = jnp.einsum('hqd,hld->hql', qn, k_own).astype(jnp.float32) * scale
        s_own = jnp.where(key_pos[None, None, :] <= q_pos[None, :, None], s_own, -jnp.inf)
        p = jax.nn.softmax(jnp.concatenate([s_sel, s_own], axis=-1), axis=-1).astype(qn.dtype)
        p_sel = p[..., :K * L].reshape(H, QC, K, L)
        p_own = p[..., K * L:]
        return (jnp.einsum('hqkl,hqkld->hqd', p_sel, v_sel)
                + jnp.einsum('hql,hld->hqd', p_own, v_own))

    out = lax.map(attend, (to_qchunks(q), to_qchunks(sel), to_qchunks(valid), b_idx, c_idx))
    out = jnp.swapaxes(out.reshape(B, NQ, H, QC, d), 1, 2)
    return out.reshape(B, H, S, d)


def hgrn2_recurrence(q, k, v, log_f):
    B, H, S, dk = q.shape
    dv = v.shape[-1]
    C = HGRN_CHUNK
    NC = S // C

    def to_chunks(t):
        return jnp.moveaxis(t.reshape(B, H, NC, C, t.shape[-1]), 2, 0)

    causal = jnp.tril(jnp.ones((C, C), dtype=bool))[:, :, None]

    def step(state, xs):
        qn, kn, vn, gn = xs
        b = jnp.cumsum(gn, axis=2)
        expo = jnp.where(causal, b[:, :, :, None, :] - b[:, :, None, :, :], -jnp.inf)
        attn = jnp.einsum('bhid,bhjd,bhijd->bhij', qn, kn, jnp.exp(expo))
        intra = jnp.einsum('bhij,bhje->bhie', attn, vn)
        cross = jnp.einsum('bhid,bhde->bhie', qn * jnp.exp(b), state)
        b_last = b[:, :, -1:, :]
        k_dec = kn * jnp.exp(b_last - b)
        new_state = (jnp.exp(b_last[:, :, 0, :])[..., None] * state
                     + jnp.einsum('bhjd,bhje->bhde', k_dec, vn))
        return new_state, intra + cross

    init = jnp.zeros((B, H, dk, dv), jnp.float32)
    _, out = lax.scan(step, init, (to_chunks(q), to_chunks(k), to_chunks(v), to_chunks(log_f)))
    return jnp.moveaxis(out, 0, 2).reshape(B, H, S, dv)


def hybrid_layer(x, lower_bound, norm_mix, w_in, ret_norm, hgrn_norm, w_branch_ret, w_branch_moba,
                 w_branch_hgrn, w_out, norm_ffn, w_ffn_gate, w_ffn_up, w_ffn_down):
    B, S, _ = x.shape
    dt = x.dtype
    f32 = jnp.float32
    pos = jnp.arange(S)
    h = rmsnorm(x, norm_mix)
    proj = h @ w_in
    (rq, rk, rv, rg, mq, mk, mv, hq, hf, hi, hg,
     gate_ret, gate_moba, gate_hgrn) = jnp.split(proj, IN_SPLITS, axis=-1)

    rq_h = rotary(split_heads(rq, RET_HEADS), pos, RET_HEAD_DIM, RET_ROPE_BASE).astype(f32)
    rk_h = rotary(split_heads(rk, RET_HEADS), pos, RET_HEAD_DIM, RET_ROPE_BASE).astype(f32)
    ret = retention(rq_h, rk_h, split_heads(rv, RET_HEADS).astype(f32))
    ret = merge_heads(head_layer_norm(ret)) * ret_norm.astype(f32)
    ret = ret.astype(dt) * jax.nn.silu(rg)

    mq_h = rotary(split_heads(mq, MOBA_HEADS), pos, ROPE_DIM, ROPE_THETA)
    mk_h = rotary(split_heads(mk, MOBA_HEADS), pos, ROPE_DIM, ROPE_THETA)
    moba = merge_heads(moba_attention(mq_h, mk_h, split_heads(mv, MOBA_HEADS)))

    f = lower_bound + (1.0 - lower_bound) * jax.nn.sigmoid(hf.astype(f32))
    hgrn = hgrn2_recurrence(split_heads(jax.nn.silu(hq.astype(f32)), HGRN_HEADS),
                            split_heads(1.0 - f, HGRN_HEADS),
                            split_heads(hi.astype(f32), HGRN_HEADS),
                            split_heads(jnp.log(f), HGRN_HEADS))
    hgrn = merge_heads(head_rms_norm(hgrn)) * hgrn_norm.astype(f32)
    hgrn = hgrn.astype(dt) * jax.nn.silu(hg)

    mixed = (jax.nn.sigmoid(gate_ret) * (ret @ w_branch_ret)
             + jax.nn.sigmoid(gate_moba) * (moba @ w_branch_moba)
             + jax.nn.sigmoid(gate_hgrn) * (hgrn @ w_branch_hgrn))
    x = x + mixed @ w_out

    h = rmsnorm(x, norm_ffn)
    x = x + (jax.nn.silu(h @ w_ffn_gate) * (h @ w_ffn_up)) @ w_ffn_down
    return x


def setup_inputs(seed: int = 0) -> dict:
    key = jax.random.key(seed)
    ks = jax.random.split(key, 16)
    f32 = jnp.float32

    def nrm(k, shape, fan_in):
        return jax.random.normal(k, shape, f32) * (fan_in ** -0.5)

    def gain(k, shape):
        return 1.0 + 0.02 * jax.random.normal(k, shape, f32)

    return {
        "x": jax.random.normal(ks[0], (BATCH, SEQ, D_MODEL), f32),
        "hgrn_lower_bounds": 0.5 * jax.random.normal(ks[1], (DEPTH, HGRN_WIDTH), f32),
        "norm_mix": gain(ks[2], (DEPTH, D_MODEL)),
        "w_in": nrm(ks[3], (DEPTH, D_MODEL, IN_COLS), D_MODEL),
        "ret_norm": gain(ks[4], (DEPTH, RET_WIDTH)),
        "hgrn_norm": gain(ks[5], (DEPTH, HGRN_WIDTH)),
        "w_branch_ret": nrm(ks[6], (DEPTH, RET_WIDTH, D_MODEL), RET_WIDTH),
        "w_branch_moba": nrm(ks[7], (DEPTH, MOBA_WIDTH, D_MODEL), MOBA_WIDTH),
        "w_branch_hgrn": nrm(ks[8], (DEPTH, HGRN_WIDTH, D_MODEL), HGRN_WIDTH),
        "w_out": nrm(ks[9], (DEPTH, D_MODEL, D_MODEL), D_MODEL),
        "norm_ffn": gain(ks[10], (DEPTH, D_MODEL)),
        "w_ffn_gate": nrm(ks[11], (DEPTH, D_MODEL, FFN_HIDDEN), D_MODEL),
        "w_ffn_up": nrm(ks[12], (DEPTH, D_MODEL, FFN_HIDDEN), D_MODEL),
        "w_ffn_down": nrm(ks[13], (DEPTH, FFN_HIDDEN, D_MODEL), FFN_HIDDEN),
        "final_norm": gain(ks[14], (D_MODEL,)),
    }


def reference(x, hgrn_lower_bounds, norm_mix, w_in, ret_norm, hgrn_norm, w_branch_ret, w_branch_moba,
              w_branch_hgrn, w_out, norm_ffn, w_ffn_gate, w_ffn_up, w_ffn_down, final_norm):
    lb_soft = jax.nn.softmax(hgrn_lower_bounds.astype(jnp.float32), axis=0)
    lower_bounds = jnp.cumsum(lb_soft, axis=0) - lb_soft[0]
    h = x
    for layer in range(DEPTH):
        h = hybrid_layer(h, lower_bounds[layer], norm_mix[layer], w_in[layer], ret_norm[layer],
                         hgrn_norm[layer], w_branch_ret[layer], w_branch_moba[layer], w_branch_hgrn[layer],
                         w_out[layer], norm_ffn[layer], w_ffn_gate[layer], w_ffn_up[layer], w_ffn_down[layer])
    return rmsnorm(h, final_norm)
```

```python
import contextlib
import numpy as np
import ml_dtypes
import concourse.bass as bass
import concourse.mybir as mybir
from concourse.bass_utils import run_bass_kernel_spmd

F32 = mybir.dt.float32
BF16 = mybir.dt.bfloat16
ALU = mybir.AluOpType
ACT = mybir.ActivationFunctionType
AX = mybir.AxisListType

D = 2048
S = 4096
NB_ = 4
DEPTH = 2
FF = 5632
EPS = 1e-6
NEG = -30000.0

ENGS = ("tensor", "vector", "scalar", "gpsimd", "sync")
SEM_LIMIT = 24000


class Prog:
    def __init__(self, nc):
        self.nc = nc
        self.stack = contextlib.ExitStack()
        self.nsem = 0
        self.ctr = {}
        self.nstage = 0

    def new_sem(self, name):
        self.nsem += 1
        return self.stack.enter_context(self.nc.semaphore(f"s{self.nsem}"))

    def bump(self, key, inc):
        c = self.ctr.get(key)
        if c is None or c[1] + inc > SEM_LIMIT:
            c = [self.new_sem(key), 0]
            self.ctr[key] = c
        c[1] += inc
        return (c[0], c[1])

    def close(self):
        self.stack.close()


class Stage:
    def __init__(self, prog, name):
        self.p = prog
        self.nc = prog.nc
        prog.nstage += 1
        self.name = f"{name}{prog.nstage}"
        self.ops = {e: [] for e in ENGS}
        self.waited = {e: {} for e in ENGS}
        self.last_w = {}
        self.readers = {}
        self.final = {}
        self.mem = contextlib.ExitStack()
        self.rr = {}

    def sbuf(self, name, shape, dtype):
        return self.mem.enter_context(self.nc.sbuf_tensor(f"{self.name}_{name}", list(shape), dtype))

    def psum(self, name, shape, dtype=F32):
        return self.mem.enter_context(self.nc.psum_tensor(f"{self.name}_{name}", list(shape), dtype))

    def pool(self, name, n, shape, dtype, psum=False):
        mk = self.psum if psum else self.sbuf
        self.rr[name] = [0, [(mk(f"{name}{i}", shape, dtype), f"{name}{i}") for i in range(n)]]

    def nxt(self, name):
        r = self.rr[name]
        t = r[1][r[0] % len(r[1])]
        r[0] += 1
        return t

    def _deps(self, reads, writes):
        toks = []
        for k in reads:
            t = self.last_w.get(k)
            if t is not None:
                toks.append(t)
        for k in writes:
            t = self.last_w.get(k)
            if t is not None:
                toks.append(t)
            toks.extend(self.readers.get(k, ()))
        return toks

    def _commit(self, tok, reads, writes):
        for k in writes:
            self.last_w[k] = tok
            self.readers[k] = []
        for k in reads:
            if k in writes:
                continue
            self.readers.setdefault(k, []).append(tok)
        self.final[id(tok[0])] = tok

    def _waits(self, eng, toks):
        need = {}
        for sem, val in toks:
            cur = self.waited[eng].get(id(sem))
            if cur is not None and cur[1] >= val:
                continue
            n = need.get(id(sem))
            if n is None or n[1] < val:
                need[id(sem)] = (sem, val)
        for k, t in need.items():
            self.waited[eng][k] = t
        return list(need.values())

    def op(self, eng, fn, reads=(), writes=()):
        waits = self._waits(eng, self._deps(reads, writes))
        tok = self.p.bump(("eng", eng), 1)
        self.ops[eng].append((waits, fn, tok, 1))
        self._commit(tok, reads, writes)

    def dma(self, q, out, in_, reads=(), writes=(), semkey=None, **kw):
        if semkey is None:
            semkey = writes[0] if writes else reads[0]
        waits = self._waits(q, self._deps(reads, writes))
        tok = self.p.bump(("dma", self.name, semkey), 16)
        self.ops[q].append((waits, lambda e: e.dma_start(out=out, in_=in_, **kw), tok, 16))
        self._commit(tok, reads, writes)

    def finish(self):
        finals = list(self.final.values())
        with self.nc.Block(self.name) as block:
            def mk(eng):
                def body(e):
                    for waits, fn, tok, inc in self.ops[eng]:
                        for sem, val in waits:
                            e.wait_ge(sem, val)
                        fn(e).then_inc(tok[0], inc)
                    for sem, val in self._waits(eng, finals):
                        e.wait_ge(sem, val)
                return body
            block.tensor(mk("tensor"))
            block.vector(mk("vector"))
            block.scalar(mk("scalar"))
            block.gpsimd(mk("gpsimd"))
            block.sync(mk("sync"))
        self.mem.close()


def fm(ap):
    return ap.rearrange("(c p) t -> p c t", p=128)


def stage_rmsnorm(prog, xT, g_ap, hT, T, out_f32=None):
    st = Stage(prog, "rms")
    KC = D // 128
    g = st.sbuf("g", [128, KC], F32)
    ones = st.sbuf("ones", [128, 128], BF16)
    st.dma("sync", g[:], g_ap, writes=["g"])
    st.op("vector", lambda e: e.memset(ones[:], 1.0), writes=["ones"])
    st.pool("x", 2, [128, KC, 512], F32)
    st.pool("sq", 2, [128, KC, 512], BF16)
    st.pool("h", 2, [128, KC, 512], F32 if out_f32 is not None else BF16)
    st.pool("r", 2, [128, 512], F32)
    st.pool("ps", 2, [128, 512], F32, psum=True)
    dst = out_f32 if out_f32 is not None else hT
    for tt in range(T // 512):
        t0 = tt * 512
        x, xk = st.nxt("x")
        sq, sqk = st.nxt("sq")
        h, hk = st.nxt("h")
        r, rk = st.nxt("r")
        ps, psk = st.nxt("ps")
        st.dma("sync", x[:], fm(xT)[:, :, t0:t0 + 512], writes=[xk])
        st.op("scalar", lambda e, x=x, sq=sq: e.activation(out=sq[:], in_=x[:], func=ACT.Square), reads=[xk], writes=[sqk])

        def mm(e, sq=sq, ps=ps):
            for k in range(KC):
                ins = e.matmul(ps[:], lhsT=ones[:], rhs=sq[:, k, :], start=(k == 0), stop=(k == KC - 1))
            return ins
        st.op("tensor", mm, reads=["ones", sqk], writes=[psk])
        st.op("scalar", lambda e, r=r, ps=ps: e.activation(out=r[:], in_=ps[:], func=ACT.Sqrt, bias=EPS, scale=1.0 / D),
              reads=[psk], writes=[rk])
        st.op("vector", lambda e, r=r: e.reciprocal(out=r[:], in_=r[:]), reads=[rk], writes=[rk])

        def nrm(e, x=x, h=h, r=r):
            for k in range(KC):
                ins = e.scalar_tensor_tensor(out=h[:, k, :], in0=x[:, k, :], scalar=g[:, k:k + 1], in1=r[:],
                                             op0=ALU.mult, op1=ALU.mult)
            return ins
        st.op("vector", nrm, reads=[xk, rk, "g"], writes=[hk])
        st.dma("sync", fm(dst)[:, :, t0:t0 + 512], h[:], reads=[hk], semkey="st" + hk)
    st.finish()


def linear_fm(prog, name, T, TT, srcs, jobs, nchunks, epi_setup, epi, G=4, wq="gpsimd"):
    st = Stage(prog, name)
    nsub = TT // 512
    acts = [st.sbuf(f"a{i}", [128, K // 128, TT], BF16) for i, (ap, K) in enumerate(srcs)]
    wsl = [[st.sbuf(f"w{j}_{s}", [128, srcs[si][1] // 128, G * 128], BF16) for s in range(2)]
           for j, (si, W) in enumerate(jobs)]
    st.pool("ps", 8, [128, 512], F32, psum=True)
    ctx = epi_setup(st)
    ng = nchunks // G
    it = 0
    for tt in range(T // TT):
        t0 = tt * TT
        for i, (ap, K) in enumerate(srcs):
            st.dma("sync", acts[i][:], fm(ap)[:, :, t0:t0 + TT], writes=[f"a{i}"])
        for g in range(ng):
            slot = it % 2
            it += 1
            for j, (si, W) in enumerate(jobs):
                st.dma(wq, wsl[j][slot][:], fm(W)[:, :, g * G * 128:(g + 1) * G * 128], writes=[f"w{j}_{slot}"])
            for cc in range(G):
                c = g * G + cc
                for sub in range(nsub):
                    pss = []
                    for j, (si, W) in enumerate(jobs):
                        ps, psk = st.nxt("ps")
                        KC = srcs[si][1] // 128

                        def mm(e, j=j, si=si, slot=slot, cc=cc, sub=sub, ps=ps, KC=KC):
                            for k in range(KC):
                                ins = e.matmul(ps[:], lhsT=wsl[j][slot][:, k, cc * 128:(cc + 1) * 128],
                                               rhs=acts[si][:, k, sub * 512:(sub + 1) * 512],
                                               start=(k == 0), stop=(k == KC - 1))
                            return ins
                        st.op("tensor", mm, reads=[f"w{j}_{slot}", f"a{si}"], writes=[psk])
                        pss.append((ps, psk))
                    epi(st, ctx, c, pss, t0 + sub * 512)
    st.finish()


def linear_tm(prog, name, T, src, K, W, N, out):
    st = Stage(prog, name)
    KC = K // 128
    act = st.sbuf("a", [128, KC, 512], BF16)
    st.pool("w", 2, [128, KC, 512], BF16)
    st.pool("ps", 4, [128, 512], F32, psum=True)
    st.pool("o", 4, [128, 512], BF16)
    for tt in range(T // 512):
        t0 = tt * 512
        st.dma("sync", act[:], fm(src)[:, :, t0:t0 + 512], writes=["a"])
        for g in range(N // 512):
            w, wk = st.nxt("w")
            st.dma("gpsimd", w[:], fm(W)[:, :, g * 512:(g + 1) * 512], writes=[wk])
            for sub in range(4):
                ps, psk = st.nxt("ps")
                o, ok = st.nxt("o")

                def mm(e, w=w, sub=sub, ps=ps):
                    for k in range(KC):
                        ins = e.matmul(ps[:], lhsT=act[:, k, sub * 128:(sub + 1) * 128], rhs=w[:, k, :],
                                       start=(k == 0), stop=(k == KC - 1))
                    return ins
                st.op("tensor", mm, reads=["a", wk], writes=[psk])
                eng = "scalar" if sub % 2 == 0 else "vector"
                if eng == "scalar":
                    st.op(eng, lambda e, o=o, ps=ps: e.copy(out=o[:], in_=ps[:]), reads=[psk], writes=[ok])
                else:
                    st.op(eng, lambda e, o=o, ps=ps: e.tensor_copy(out=o[:], in_=ps[:]), reads=[psk], writes=[ok])
                st.dma("sync", out[t0 + sub * 128:t0 + (sub + 1) * 128, g * 512:(g + 1) * 512], o[:], reads=[ok],
                       semkey="st" + ok)
    st.finish()


def epi_store_f32(out):
    def setup(st):
        st.pool("eo", 4, [128, 512], F32)
        return {"n": 0}

    def epi(st, ctx, c, pss, tok0):
        ps, psk = pss[0]
        o, ok = st.nxt("eo")
        ctx["n"] += 1
        if ctx["n"] % 2 == 0:
            st.op("scalar", lambda e: e.copy(out=o[:], in_=ps[:]), reads=[psk], writes=[ok])
        else:
            st.op("vector", lambda e: e.tensor_copy(out=o[:], in_=ps[:]), reads=[psk], writes=[ok])
        st.dma("sync", out[c * 128:(c + 1) * 128, tok0:tok0 + 512], o[:], reads=[ok], semkey="st" + ok)
    return setup, epi


def stage_retention(prog, pfm, ptm, C, mixT, qrow, krow, grow, vcol, orow):
    st = Stage(prog, "ret")
    idb = st.sbuf("idb", [128, 128], BF16)
    onesf = st.sbuf("onesf", [128, 128], F32)
    st.dma("gpsimd", idb[:], C["ident"], writes=["idb"])
    st.op("vector", lambda e: e.memset(onesf[:], 1.0), writes=["onesf"])
    cosr = st.sbuf("cos", [128, 512], F32)
    sinr = st.sbuf("sin", [128, 512], F32)
    Mt = st.sbuf("M", [128, 2, 512], F32)
    qd = st.sbuf("qd", [128, 2, 512], F32)
    kd = st.sbuf("kd", [128, 2, 4], F32)
    g5 = st.sbuf("g5", [128, 2, 1], F32)
    rn = st.sbuf("rn", [128, 4], F32)
    st.dma("sync", Mt[:], C["retM"], writes=["M"])
    st.dma("sync", qd[:], C["retqd"], writes=["qd"])
    st.dma("sync", kd[:], C["retkd"], writes=["kd"])
    st.dma("sync", g5[:], C["retg5"], writes=["g5"])
    st.dma("sync", rn[:], C["retn"], writes=["rn"])
    Rf = [st.sbuf(f"Rf{h}", [128, 2, 256], F32) for h in range(2)]
    Rb = [st.sbuf(f"Rb{h}", [128, 2, 256], BF16) for h in range(2)]
    st.pool("qk", 2, [128, 4, 512], F32)
    st.pool("gt", 2, [128, 2, 512], F32)
    st.pool("t", 4, [128, 512], F32)
    st.pool("QT", 2, [128, 2, 512], BF16)
    st.pool("KT", 2, [128, 2, 512], BF16)
    st.pool("QW", 2, [128, 2, 512], BF16)
    st.pool("V", 2, [128, 4, 256], BF16)
    st.pool("KW", 2, [128, 4, 256], BF16)
    st.pool("PT", 3, [128, 512], BF16)
    st.pool("os", 2, [128, 2, 512], F32)
    st.pool("osq", 2, [128, 2, 512], F32)
    st.pool("st", 2, [128, 4, 512], F32)
    st.pool("y", 2, [128, 2, 512], BF16)
    st.pool("pS", 2, [128, 512], F32, psum=True)
    st.pool("pO", 2, [128, 2, 512], F32, psum=True)
    st.pool("pT", 1, [128, 4, 128], BF16, psum=True)
    st.pool("pR", 1, [128, 512], F32, psum=True)

    for n in range(S // 512):
        t0 = n * 512
        st.dma("sync", cosr[:], C["cosR"][:, t0:t0 + 512], writes=["cos"])
        st.dma("sync", sinr[:], C["sinR"][:, t0:t0 + 512], writes=["sin"])
        for h in range(2):
            qk, qkk = st.nxt("qk")
            gt, gtk = st.nxt("gt")
            QT, QTk = st.nxt("QT")
            KT, KTk = st.nxt("KT")
            QW, QWk = st.nxt("QW")
            V, Vk = st.nxt("V")
            KW, KWk = st.nxt("KW")
            for i, row in enumerate((qrow, krow)):
                st.dma("sync", qk[:, 2 * i:2 * i + 2, :], fm(pfm[row + h * 256:row + (h + 1) * 256, :])[:, :, t0:t0 + 512],
                       writes=[qkk])
            st.dma("sync", gt[:], fm(pfm[grow + h * 256:grow + (h + 1) * 256, :])[:, :, t0:t0 + 512], writes=[gtk])
            st.dma("sync", V[:], ptm[t0:t0 + 512, vcol + h * 256:vcol + (h + 1) * 256].rearrange("(j p) e -> p j e", p=128),
                   writes=[Vk])
            st.op("scalar", lambda e, gt=gt: e.activation(out=gt[:], in_=gt[:], func=ACT.Silu), reads=[gtk], writes=[gtk])
            for i, (dst, scale) in enumerate(((QT, 1.0), (KT, 1.0 / 16.0))):
                x0 = qk[:, 2 * i, :]
                x1 = qk[:, 2 * i + 1, :]
                ta, tak = st.nxt("t")
                tb, tbk = st.nxt("t")
                st.op("vector", lambda e, x0=x0, ta=ta: e.tensor_tensor(out=ta[:], in0=x0, in1=cosr[:], op=ALU.mult),
                      reads=[qkk, "cos"], writes=[tak])
                st.op("gpsimd", lambda e, x1=x1, tb=tb: e.tensor_tensor(out=tb[:], in0=x1, in1=sinr[:], op=ALU.mult),
                      reads=[qkk, "sin"], writes=[tbk])
                if scale == 1.0:
                    st.op("vector", lambda e, ta=ta, tb=tb, dst=dst: e.tensor_tensor(out=dst[:, 0, :], in0=ta[:], in1=tb[:], op=ALU.subtract),
                          reads=[tak, tbk], writes=[QTk + "0"])
                else:
                    st.op("vector", lambda e, ta=ta, tb=tb, dst=dst: e.tensor_tensor(out=ta[:], in0=ta[:], in1=tb[:], op=ALU.subtract),
                          reads=[tak, tbk], writes=[tak])
                    st.op("scalar", lambda e, ta=ta, dst=dst, scale=scale: e.mul(out=dst[:, 0, :], in_=ta[:], mul=scale),
                          reads=[tak], writes=[KTk + "0"])
                tc_, tck = st.nxt("t")
                td, tdk = st.nxt("t")
                st.op("vector", lambda e, x1=x1, tc_=tc_: e.tensor_tensor(out=tc_[:], in0=x1, in1=cosr[:], op=ALU.mult),
                      reads=[qkk, "cos"], writes=[tck])
                st.op("gpsimd", lambda e, x0=x0, td=td: e.tensor_tensor(out=td[:], in0=x0, in1=sinr[:], op=ALU.mult),
                      reads=[qkk, "sin"], writes=[tdk])
                if scale == 1.0:
                    st.op("vector", lambda e, tc_=tc_, td=td, dst=dst: e.tensor_tensor(out=dst[:, 1, :], in0=tc_[:], in1=td[:], op=ALU.add),
                          reads=[tck, tdk], writes=[QTk + "1"])
                else:
                    st.op("vector", lambda e, tc_=tc_, td=td: e.tensor_tensor(out=tc_[:], in0=tc_[:], in1=td[:], op=ALU.add),
                          reads=[tck, tdk], writes=[tck])
                    st.op("scalar", lambda e, tc_=tc_, dst=dst, scale=scale: e.mul(out=dst[:, 1, :], in_=tc_[:], mul=scale),
                          reads=[tck], writes=[KTk + "1"])
            QTr = [QTk + "0", QTk + "1"]
            KTr = [KTk + "0", KTk + "1"]
            if n > 0:
                for c in range(2):
                    st.op("gpsimd", lambda e, c=c, QW=QW, QT=QT, h=h: e.tensor_tensor(
                        out=QW[:, c, :], in0=QT[:, c, :], in1=qd[:, h, :], op=ALU.mult), reads=QTr + ["qd"], writes=[QWk])
            pT, pTk = st.nxt("pT")
            for c in range(2):
                def tr(e, c=c, KT=KT, pT=pT):
                    for jt in range(4):
                        ins = e.transpose(out=pT[:, jt, :], in_=KT[:, c, jt * 128:(jt + 1) * 128], identity=idb[:])
                    return ins
                st.op("tensor", tr, reads=KTr + ["idb"], writes=[pTk])

                def ev(e, c=c, KW=KW, pT=pT, h=h):
                    for jt in range(4):
                        ins = e.tensor_scalar_mul(out=KW[:, jt, c * 128:(c + 1) * 128], in0=pT[:, jt, :],
                                                  scalar1=kd[:, h, jt:jt + 1])
                    return ins
                st.op("vector", ev, reads=[pTk, "kd"], writes=[KWk])
            pO, pOk = st.nxt("pO")
            first = [True, True]
            if n > 0:
                for ec in range(2):
                    def cross(e, ec=ec, pO=pO, QW=QW, h=h):
                        for c in range(2):
                            ins = e.matmul(pO[:, ec, :], lhsT=Rb[h][:, c, ec * 128:(ec + 1) * 128], rhs=QW[:, c, :],
                                           start=(c == 0), stop=False)
                        return ins
                    st.op("tensor", cross, reads=[QWk, f"Rb{h}"], writes=[pOk])
                    first[ec] = False
            for jt in range(4):
                N = 512 - 128 * jt
                pS, pSk = st.nxt("pS")
                PT, PTk = st.nxt("PT")

                def sc(e, jt=jt, N=N, pS=pS, KT=KT, QT=QT):
                    for c in range(2):
                        ins = e.matmul(pS[:, 0:N], lhsT=KT[:, c, jt * 128:(jt + 1) * 128], rhs=QT[:, c, 128 * jt:512],
                                       start=(c == 0), stop=(c == 1))
                    return ins
                st.op("tensor", sc, reads=KTr + QTr, writes=[pSk])
                st.op("vector", lambda e, N=N, PT=PT, pS=pS, h=h: e.tensor_tensor(out=PT[:, 0:N], in0=pS[:, 0:N], in1=Mt[:, h, 0:N], op=ALU.mult),
                      reads=[pSk, "M"], writes=[PTk])

                def pv(e, jt=jt, N=N, PT=PT, V=V, pO=pO, fst=(first[0] and jt == 0)):
                    for ec in range(2):
                        ins = e.matmul(pO[:, ec, 128 * jt:512], lhsT=V[:, jt, ec * 128:(ec + 1) * 128], rhs=PT[:, 0:N],
                                       start=fst, stop=(jt == 3))
                    return ins
                st.op("tensor", pv, reads=[PTk, Vk], writes=[pOk])
            if n < S // 512 - 1:
                for c in range(2):
                    pR, pRk = st.nxt("pR")

                    def su(e, c=c, KW=KW, V=V, pR=pR):
                        for jt in range(4):
                            ins = e.matmul(pR[:, 0:256], lhsT=KW[:, jt, c * 128:(c + 1) * 128], rhs=V[:, jt, :],
                                           start=(jt == 0), stop=(jt == 3))
                        return ins
                    st.op("tensor", su, reads=[KWk, Vk], writes=[pRk])
                    if n == 0:
                        st.op("vector", lambda e, c=c, pR=pR, h=h: e.tensor_copy(out=Rf[h][:, c, :], in_=pR[:, 0:256]),
                              reads=[pRk], writes=[f"Rf{h}{c}"])
                    else:
                        st.op("vector", lambda e, c=c, pR=pR, h=h: e.scalar_tensor_tensor(
                            out=Rf[h][:, c, :], in0=Rf[h][:, c, :], scalar=g5[:, h, 0:1], in1=pR[:, 0:256],
                            op0=ALU.mult, op1=ALU.add), reads=[pRk, f"Rf{h}{c}", "g5"], writes=[f"Rf{h}{c}"])
                st.op("scalar", lambda e, h=h: e.copy(out=Rb[h][:], in_=Rf[h][:]), reads=[f"Rf{h}0", f"Rf{h}1"], writes=[f"Rb{h}"])
            os_, osk = st.nxt("os")
            osq, osqk = st.nxt("osq")
            sx, sxk = st.nxt("st")
            y, yk = st.nxt("y")
            st.op("scalar", lambda e, os_=os_, pO=pO: e.copy(out=os_[:], in_=pO[:]), reads=[pOk], writes=[osk])
            st.op("gpsimd", lambda e, os_=os_, osq=osq: e.tensor_tensor(out=osq[:], in0=os_[:], in1=os_[:], op=ALU.mult),
                  reads=[osk], writes=[osqk])
            p1, p1k = st.nxt("pS")
            p2, p2k = st.nxt("pR")

            def s1(e, os_=os_, p1=p1):
                for ec in range(2):
                    ins = e.matmul(p1[:], lhsT=onesf[:], rhs=os_[:, ec, :], start=(ec == 0), stop=(ec == 1))
                return ins
            st.op("tensor", s1, reads=[osk, "onesf"], writes=[p1k])

            def s2(e, osq=osq, p2=p2):
                for ec in range(2):
                    ins = e.matmul(p2[:], lhsT=onesf[:], rhs=osq[:, ec, :], start=(ec == 0), stop=(ec == 1))
                return ins
            st.op("tensor", s2, reads=[osqk, "onesf"], writes=[p2k])
            mean = sx[:, 0, :]
            var = sx[:, 1, :]
            st.op("scalar", lambda e, mean=mean, p1=p1: e.mul(out=mean, in_=p1[:], mul=1.0 / 256.0), reads=[p1k], writes=[sxk + "m"])
            st.op("vector", lambda e, mean=mean, sx=sx: e.tensor_tensor(out=sx[:, 2, :], in0=mean, in1=mean, op=ALU.mult),
                  reads=[sxk + "m"], writes=[sxk + "q"])
            st.op("vector", lambda e, var=var, p2=p2, sx=sx: e.scalar_tensor_tensor(
                out=var, in0=p2[:], scalar=1.0 / 256.0, in1=sx[:, 2, :], op0=ALU.mult, op1=ALU.subtract),
                reads=[p2k, sxk + "q"], writes=[sxk + "v"])
            st.op("scalar", lambda e, var=var: e.activation(out=var, in_=var, func=ACT.Sqrt, bias=EPS, scale=1.0),
                  reads=[sxk + "v"], writes=[sxk + "v"])
            st.op("vector", lambda e, var=var: e.reciprocal(out=var, in_=var), reads=[sxk + "v"], writes=[sxk + "v"])
            for ec in range(2):
                eng = "vector" if ec == 0 else "gpsimd"
                st.op(eng, lambda e, ec=ec, os_=os_, mean=mean: e.tensor_tensor(out=os_[:, ec, :], in0=os_[:, ec, :], in1=mean, op=ALU.subtract),
                      reads=[sxk + "m"], writes=[osk])
                st.op(eng, lambda e, ec=ec, os_=os_, var=var: e.tensor_tensor(out=os_[:, ec, :], in0=os_[:, ec, :], in1=var, op=ALU.mult),
                      reads=[sxk + "v"], writes=[osk])
                st.op("vector", lambda e, ec=ec, os_=os_, y=y, gt=gt, h=h: e.scalar_tensor_tensor(
                    out=y[:, ec, :], in0=os_[:, ec, :], scalar=rn[:, h * 2 + ec:h * 2 + ec + 1], in1=gt[:, ec, :],
                    op0=ALU.mult, op1=ALU.mult), reads=[osk, gtk, "rn"], writes=[yk])
            st.dma("sync", fm(mixT[orow + h * 256:orow + (h + 1) * 256, :])[:, :, t0:t0 + 512], y[:], reads=[yk], semkey="st" + yk)
    st.finish()


def stage_moba(prog, pfm, ptm, C, mixT, qrow, krow, vcol, orow):
    st = Stage(prog, "moba")
    NT = S // 512
    idb = st.sbuf("idb", [128, 128], BF16)
    idf = st.sbuf("idf", [128, 128], F32)
    onesb = st.sbuf("onesb", [128, 128], BF16)
    RTb = st.sbuf("RTb", [128, 128], BF16)
    pastb = st.sbuf("pastb", [128, 256], F32)
    ownfix = st.sbuf("ownfix", [128, 256], F32)
    Eoh = st.sbuf("Eoh", [16, 2048], BF16)
    causb = st.sbuf("causb", [128, 4, 512], BF16)
    cosm = st.sbuf("cosm", [128, S], F32)
    sinm = st.sbuf("sinm", [128, S], F32)
    st.dma("gpsimd", idb[:], C["ident"], writes=["idb"])
    st.dma("sync", idf[:], C["ident"], writes=["idf"])
    st.dma("gpsimd", RTb[:], C["RT"], writes=["RTb"])
    st.dma("sync", pastb[:], C["pastb"], writes=["pastb"])
    st.dma("sync", ownfix[:], C["ownfix"], writes=["ownfix"])
    st.dma("gpsimd", Eoh[:], C["Eoh"], writes=["Eoh"])
    st.dma("gpsimd", causb[:], C["causb"].rearrange("j p t -> p j t"), writes=["causb"])
    st.dma("sync", cosm[:], C["cosM"], writes=["cosm"])
    st.dma("sync", sinm[:], C["sinM"], writes=["sinm"])
    st.op("vector", lambda e: e.memset(onesb[:], 1.0), writes=["onesb"])
    QTa = st.sbuf("QTa", [128, S], BF16)
    KTa = st.sbuf("KTa", [128, S], BF16)
    Va = st.sbuf("Va", [128, S // 128, 128], BF16)
    NMT = st.sbuf("NMT", [16, S], BF16)
    kmean = st.sbuf("kmean", [128, 16], F32)
    st.pool("raw", 2, [128, 2, 512], F32)
    st.pool("rb", 2, [128, 2, 512], BF16)
    st.pool("t", 4, [128, 512], F32)
    st.pool("rot", 2, [128, 2, 512], F32)
    st.pool("gm", 2, [128, 4, 16], F32)
    st.pool("nm", 2, [128, 4, 16], F32)
    st.pool("m8", 2, [128, 4, 8], F32)
    st.pool("PT", 3, [128, 512], BF16)
    st.pool("rz", 2, [128, 512], F32)
    st.pool("o", 2, [128, 512], BF16)
    st.pool("pS", 2, [128, 512], F32, psum=True)
    st.pool("pO", 1, [128, 512], F32, psum=True)
    st.pool("pZ", 1, [128, 512], F32, psum=True)
    st.pool("pr", 2, [128, 512], F32, psum=True)
    st.pool("pg", 1, [128, 4, 16], F32, psum=True)
    st.pool("ptr", 1, [16, 512], F32, psum=True)
    scale = 128.0 ** -0.5
    for h in range(4):
        st.op("vector", lambda e: e.memset(kmean[:], 0.0), writes=["kmean"])
        for n in range(NT):
            t0 = n * 512
            raw, rawk = st.nxt("raw")
            rb, rbk = st.nxt("rb")
            rot, rotk = st.nxt("rot")
            st.dma("sync", raw[:, 0, :], pfm[qrow + h * 128:qrow + (h + 1) * 128, t0:t0 + 512], writes=[rawk])
            st.dma("sync", raw[:, 1, :], pfm[krow + h * 128:krow + (h + 1) * 128, t0:t0 + 512], writes=[rawk])
            st.dma("sync", Va[:, 4 * n:4 * n + 4, :],
                   ptm[t0:t0 + 512, vcol + h * 128:vcol + (h + 1) * 128].rearrange("(j p) e -> p j e", p=128), writes=["Va"])
            st.op("scalar", lambda e, rb=rb, raw=raw: e.copy(out=rb[:], in_=raw[:]), reads=[rawk], writes=[rbk])
            for i in range(2):
                pr, prk = st.nxt("pr")
                ta, tak = st.nxt("t")
                tb, tbk = st.nxt("t")
                st.op("tensor", lambda e, i=i, pr=pr, rb=rb: e.matmul(pr[:], lhsT=RTb[:], rhs=rb[:, i, :], start=True, stop=True),
                      reads=[rbk, "RTb"], writes=[prk])
                st.op("gpsimd", lambda e, i=i, ta=ta, raw=raw, t0=t0: e.tensor_tensor(out=ta[:], in0=raw[:, i, :], in1=cosm[:, t0:t0 + 512], op=ALU.mult),
                      reads=[rawk, "cosm"], writes=[tak])
                st.op("vector", lambda e, tb=tb, pr=pr, t0=t0: e.tensor_tensor(out=tb[:], in0=pr[:], in1=sinm[:, t0:t0 + 512], op=ALU.mult),
                      reads=[prk, "sinm"], writes=[tbk])
                st.op("gpsimd", lambda e, i=i, ta=ta, tb=tb, rot=rot: e.tensor_tensor(out=rot[:, i, :], in0=ta[:], in1=tb[:], op=ALU.add),
                      reads=[tak, tbk], writes=[rotk + str(i)])
            st.op("scalar", lambda e, rot=rot, t0=t0: e.copy(out=QTa[:, t0:t0 + 512], in_=rot[:, 0, :]), reads=[rotk + "0"], writes=["QTa"])
            st.op("scalar", lambda e, rot=rot, t0=t0: e.copy(out=KTa[:, t0:t0 + 512], in_=rot[:, 1, :]), reads=[rotk + "1"], writes=["KTa"])
            st.op("vector", lambda e, rot=rot, n=n: e.tensor_reduce(out=kmean[:, 2 * n:2 * n + 2],
                                                                   in_=rot[:, 1, :].rearrange("p (b l) -> p b l", l=256),
                                                                   axis=AX.X, op=ALU.add), reads=[rotk + "1"], writes=["kmean"])
            st.op("vector", lambda e, n=n: e.tensor_scalar_mul(out=kmean[:, 2 * n:2 * n + 2], in0=kmean[:, 2 * n:2 * n + 2], scalar1=1.0 / 256.0),
                  reads=[], writes=["kmean"])
            pg, pgk = st.nxt("pg")
            gm, gmk = st.nxt("gm")
            nm, nmk = st.nxt("nm")
            m8, m8k = st.nxt("m8")
            ptr, ptrk = st.nxt("ptr")

            def gate(e, rot=rot, pg=pg):
                for sub in range(4):
                    ins = e.matmul(pg[:, sub, :], lhsT=rot[:, 0, sub * 128:(sub + 1) * 128], rhs=kmean[:], start=True, stop=True)
                return ins
            st.op("tensor", gate, reads=[rotk + "0", "kmean"], writes=[pgk])

            def sel(e, n=n, pg=pg, gm=gm, nm=nm, m8=m8):
                for sub in range(4):
                    qb = 2 * n + sub // 2
                    e.tensor_tensor(out=gm[:, sub, :], in0=pg[:, sub, :], in1=pastb[:, qb * 16:(qb + 1) * 16], op=ALU.add)
                    e.max(out=m8[:, sub, :], in_=gm[:, sub, :])
                    e.tensor_scalar_max(out=m8[:, sub, 2:3], in0=m8[:, sub, 2:3], scalar1=-1e29)
                    e.tensor_scalar(out=nm[:, sub, :], in0=gm[:, sub, :], scalar1=m8[:, sub, 2:3], scalar2=NEG,
                                    op0=ALU.is_lt, op1=ALU.mult)
                    ins = e.tensor_tensor(out=nm[:, sub, :], in0=nm[:, sub, :], in1=ownfix[:, qb * 16:(qb + 1) * 16], op=ALU.max)
                return ins
            for sub in range(4):
                qb = 2 * n + sub // 2
                st.op("vector", lambda e, sub=sub, qb=qb, gm=gm, pg=pg: e.tensor_tensor(
                    out=gm[:, sub, :], in0=pg[:, sub, :], in1=pastb[:, qb * 16:(qb + 1) * 16], op=ALU.add),
                    reads=[pgk, "pastb"], writes=[gmk])
                st.op("vector", lambda e, sub=sub, gm=gm, m8=m8: e.max(out=m8[:, sub, :], in_=gm[:, sub, :]), reads=[gmk], writes=[m8k])
                st.op("vector", lambda e, sub=sub, m8=m8: e.tensor_scalar_max(out=m8[:, sub, 2:3], in0=m8[:, sub, 2:3], scalar1=-1e29),
                      reads=[m8k], writes=[m8k])
                st.op("vector", lambda e, sub=sub, gm=gm, nm=nm, m8=m8: e.tensor_scalar(
                    out=nm[:, sub, :], in0=gm[:, sub, :], scalar1=m8[:, sub, 2:3], scalar2=NEG, op0=ALU.is_lt, op1=ALU.mult),
                    reads=[gmk, m8k], writes=[nmk])
                st.op("vector", lambda e, sub=sub, qb=qb, nm=nm: e.tensor_tensor(
                    out=nm[:, sub, :], in0=nm[:, sub, :], in1=ownfix[:, qb * 16:(qb + 1) * 16], op=ALU.max),
                    reads=["ownfix"], writes=[nmk])

            def trn(e, nm=nm, ptr=ptr):
                for sub in range(4):
                    ins = e.transpose(out=ptr[:, sub * 128:(sub + 1) * 128], in_=nm[:, sub, :], identity=idf[:])
                return ins
            st.op("tensor", trn, reads=[nmk, "idf"], writes=[ptrk])
            st.op("scalar", lambda e, ptr=ptr, t0=t0: e.copy(out=NMT[:, t0:t0 + 512], in_=ptr[:]), reads=[ptrk], writes=["NMT"])
        for n in range(NT):
            t0 = n * 512
            pO, pOk = st.nxt("pO")
            pZ, pZk = st.nxt("pZ")
            last = 4 * n + 3
            for kt in range(last + 1):
                pS, pSk = st.nxt("pS")
                PT, PTk = st.nxt("PT")
                diag = kt >= 4 * n

                def sc(e, kt=kt, t0=t0, pS=pS, diag=diag, n=n):
                    e.matmul(pS[:], lhsT=KTa[:, kt * 128:(kt + 1) * 128], rhs=QTa[:, t0:t0 + 512], start=True, stop=False)
                    blk = kt // 2
                    ins = e.matmul(pS[:], lhsT=Eoh[:, blk * 128:(blk + 1) * 128], rhs=NMT[:, t0:t0 + 512], start=False, stop=(not diag))
                    if diag:
                        ins = e.matmul(pS[:], lhsT=idb[:], rhs=causb[:, kt - 4 * n, :], start=False, stop=True)
                    return ins
                st.op("tensor", sc, reads=["KTa", "QTa", "NMT", "Eoh", "idb", "causb"], writes=[pSk])
                st.op("scalar", lambda e, PT=PT, pS=pS: e.activation(out=PT[:], in_=pS[:], func=ACT.Exp, scale=scale),
                      reads=[pSk], writes=[PTk])

                def pv(e, kt=kt, PT=PT, pO=pO, pZ=pZ, last=last):
                    e.matmul(pO[:], lhsT=Va[:, kt, :], rhs=PT[:], start=(kt == 0), stop=(kt == last))
                    return e.matmul(pZ[:], lhsT=onesb[:], rhs=PT[:], start=(kt == 0), stop=(kt == last))
                st.op("tensor", pv, reads=[PTk, "Va", "onesb"], writes=[pOk, pZk])
            rz, rzk = st.nxt("rz")
            o, ok = st.nxt("o")
            st.op("vector", lambda e, rz=rz, pZ=pZ: e.reciprocal(out=rz[:], in_=pZ[:]), reads=[pZk], writes=[rzk])
            st.op("vector", lambda e, rz=rz, pO=pO, o=o: e.tensor_tensor(out=o[:], in0=pO[:], in1=rz[:], op=ALU.mult),
                  reads=[pOk, rzk], writes=[ok])
            st.dma("sync", mixT[orow + h * 128:orow + (h + 1) * 128, t0:t0 + 512], o[:], reads=[ok], semkey="st" + ok)
    st.finish()


def stage_hgrn(prog, pfm, ptm, C, mixT, qrow, frow, grow, vcol, orow, layer):
    st = Stage(prog, "hgrn")
    NT = S // 512
    idb = st.sbuf("idb", [128, 128], BF16)
    onesf = st.sbuf("onesf", [128, 128], F32)
    seg = st.sbuf("seg", [128, 512], F32)
    tril = st.sbuf("tril", [64, 512], F32)
    hlb = st.sbuf("hlb", [128, 8], F32)
    lb = st.sbuf("lb", [128, 4], F32)
    oml = st.sbuf("oml", [128, 4], F32)
    hgn = st.sbuf("hgn", [128, 4], F32)
    st.dma("gpsimd", idb[:], C["ident"], writes=["idb"])
    st.dma("sync", seg[:], C["segmask"], writes=["seg"])
    st.dma("sync", tril[:], C["tril"], writes=["tril"])
    st.dma("sync", hlb[:], C["hlb"], writes=["hlb"])
    st.dma("sync", hgn[:], C["hgn"], writes=["hgn"])
    st.op("vector", lambda e: e.memset(onesf[:], 1.0), writes=["onesf"])
    if layer == 0:
        st.op("vector", lambda e: e.memset(lb[:], 0.0), writes=["lb"])
    else:
        hl3 = hlb[:].rearrange("p (h i) -> p h i", i=2)
        st.op("vector", lambda e: e.tensor_tensor(out=lb[:], in0=hl3[:, :, 1], in1=hl3[:, :, 0], op=ALU.subtract), reads=["hlb"], writes=["lb"])
        st.op("scalar", lambda e: e.activation(out=lb[:], in_=lb[:], func=ACT.Sigmoid), reads=["lb"], writes=["lb"])
    st.op("vector", lambda e: e.tensor_scalar(out=oml[:], in0=lb[:], scalar1=-1.0, scalar2=1.0, op0=ALU.mult, op1=ALU.add),
          reads=["lb"], writes=["oml"])
    Sf = st.sbuf("Sf", [128, 128], F32)
    st.pool("Sb", 3, [128, 128], BF16)
    st.pool("raw", 2, [128, 3, 512], F32)
    st.pool("V", 2, [64, 8, 128], BF16)
    st.pool("f", 2, [128, 512], F32)
    st.pool("kk", 2, [128, 512], F32)
    st.pool("b", 2, [128, 512], F32)
    st.pool("E", 4, [128, 512], F32)
    st.pool("nr", 2, [128, 8], F32)
    st.pool("Q1", 2, [128, 512], BF16)
    st.pool("K1", 2, [128, 512], BF16)
    st.pool("Qb", 2, [128, 512], BF16)
    st.pool("Kd", 2, [128, 512], BF16)
    st.pool("KdT", 2, [64, 8, 128], BF16)
    st.pool("P", 2, [64, 512], BF16)
    st.pool("osq", 2, [128, 512], F32)
    st.pool("rs", 2, [128, 512], F32)
    st.pool("y", 2, [128, 512], F32)
    st.pool("yb", 2, [128, 512], BF16)
    st.pool("pS", 1, [64, 512], F32, psum=True)
    st.pool("pT", 1, [64, 8, 128], BF16, psum=True)
    st.pool("pU", 2, [128, 128], F32, psum=True)
    st.pool("pO", 2, [128, 512], F32, psum=True)
    st.pool("p2", 1, [128, 512], F32, psum=True)
    for h in range(4):
        Sb, Sbk = None, None
        for n in range(NT):
            t0 = n * 512
            raw, rawk = st.nxt("raw")
            V, Vk = st.nxt("V")
            for i, row in enumerate((qrow, frow, grow)):
                st.dma("sync", raw[:, i, :], pfm[row + h * 128:row + (h + 1) * 128, t0:t0 + 512], writes=[rawk + str(i)])
            st.dma("sync", V[:], ptm[t0:t0 + 512, vcol + h * 128:vcol + (h + 1) * 128].rearrange("(c j) e -> j c e", j=64),
                   writes=[Vk])
            f, fk = st.nxt("f")
            kk, kkk = st.nxt("kk")
            b, bk = st.nxt("b")
            q = raw[:, 0, :]
            st.op("scalar", lambda e, raw=raw: e.activation(out=raw[:, 0, :], in_=raw[:, 0, :], func=ACT.Silu), reads=[], writes=[rawk + "0"])
            st.op("scalar", lambda e, raw=raw: e.activation(out=raw[:, 2, :], in_=raw[:, 2, :], func=ACT.Silu), reads=[], writes=[rawk + "2"])
            st.op("scalar", lambda e, raw=raw, f=f: e.activation(out=f[:], in_=raw[:, 1, :], func=ACT.Sigmoid), reads=[rawk + "1"], writes=[fk])
            st.op("vector", lambda e, f=f, h=h: e.tensor_scalar(out=f[:], in0=f[:], scalar1=oml[:, h:h + 1], scalar2=lb[:, h:h + 1],
                                                                 op0=ALU.mult, op1=ALU.add), reads=["oml", "lb"], writes=[fk])
            st.op("gpsimd", lambda e, f=f, kk=kk: e.tensor_scalar(out=kk[:], in0=f[:], scalar1=-1.0, scalar2=1.0, op0=ALU.mult, op1=ALU.add),
                  reads=[fk], writes=[kkk])
            st.op("scalar", lambda e, f=f: e.activation(out=f[:], in_=f[:], func=ACT.Ln), reads=[kkk], writes=[fk])
            st.op("vector", lambda e, f=f, b=b: e.tensor_tensor_scan(out=b[:], data0=seg[:], data1=f[:], initial=0.0, op0=ALU.mult, op1=ALU.add),
                  reads=[fk, "seg"], writes=[bk])
            b3 = b[:].rearrange("p (c t) -> p c t", t=64)
            nr, nrk = st.nxt("nr")
            st.op("vector", lambda e, nr=nr, b3=b3: e.tensor_scalar_mul(out=nr[:], in0=b3[:, :, 32], scalar1=-1.0), reads=[bk], writes=[nrk])
            E1, E1k = st.nxt("E")
            E2, E2k = st.nxt("E")
            E3, E3k = st.nxt("E")
            E4, E4k = st.nxt("E")

            def ex(e, b=b, b3=b3, nr=nr, E1=E1, E2=E2, E3=E3, E4=E4):
                e.activation(out=E3[:], in_=b[:], func=ACT.Exp)
                for c in range(8):
                    cs = slice(c * 64, (c + 1) * 64)
                    e.activation(out=E1[:, cs], in_=b[:, cs], func=ACT.Exp, bias=nr[:, c:c + 1], scale=1.0)
                    e.activation(out=E2[:, cs], in_=b[:, cs], func=ACT.Exp, bias=b3[:, c, 32:33], scale=-1.0)
                    ins = e.activation(out=E4[:, cs], in_=b[:, cs], func=ACT.Exp, bias=b3[:, c, 63:64], scale=-1.0)
                return ins
            st.op("scalar", ex, reads=[bk, nrk], writes=[E1k, E2k, E3k, E4k])
            Q1, Q1k = st.nxt("Q1")
            K1, K1k = st.nxt("K1")
            Qb, Qbk = st.nxt("Qb")
            Kd, Kdk = st.nxt("Kd")
            st.op("vector", lambda e, Q1=Q1, q=q, E1=E1: e.tensor_tensor(out=Q1[:], in0=q, in1=E1[:], op=ALU.mult), reads=[rawk + "0", E1k], writes=[Q1k])
            st.op("gpsimd", lambda e, K1=K1, kk=kk, E2=E2: e.tensor_tensor(out=K1[:], in0=kk[:], in1=E2[:], op=ALU.mult), reads=[kkk, E2k], writes=[K1k])
            st.op("vector", lambda e, Qb=Qb, q=q, E3=E3: e.tensor_tensor(out=Qb[:], in0=q, in1=E3[:], op=ALU.mult), reads=[rawk + "0", E3k], writes=[Qbk])
            st.op("gpsimd", lambda e, Kd=Kd, kk=kk, E4=E4: e.tensor_tensor(out=Kd[:], in0=kk[:], in1=E4[:], op=ALU.mult), reads=[kkk, E4k], writes=[Kdk])
            pT, pTk = st.nxt("pT")
            KdT, KdTk = st.nxt("KdT")

            def tr(e, Kd=Kd, pT=pT):
                for c in range(8):
                    ins = e.transpose(out=pT[:, c, :], in_=Kd[:, c * 64:(c + 1) * 64], identity=idb[:])
                return ins
            st.op("tensor", tr, reads=[Kdk, "idb"], writes=[pTk])
            st.op("scalar", lambda e, KdT=KdT, pT=pT: e.copy(out=KdT[:], in_=pT[:]), reads=[pTk], writes=[KdTk])
            pS, pSk = st.nxt("pS")
            P, Pk = st.nxt("P")

            def sc(e, K1=K1, Q1=Q1, pS=pS):
                for c in range(8):
                    cs = slice(c * 64, (c + 1) * 64)
                    ins = e.matmul(pS[:, cs], lhsT=K1[:, cs], rhs=Q1[:, cs], start=True, stop=True)
                return ins
            st.op("tensor", sc, reads=[K1k, Q1k], writes=[pSk])
            st.op("vector", lambda e, P=P, pS=pS: e.tensor_tensor(out=P[:], in0=pS[:], in1=tril[:], op=ALU.mult), reads=[pSk, "tril"], writes=[Pk])
            pO, pOk = st.nxt("pO")
            for c in range(8):
                cs = slice(c * 64, (c + 1) * 64)
                have_state = not (n == 0 and c == 0)

                def oc(e, c=c, cs=cs, have_state=have_state, Sb=Sb, Qb=Qb, V=V, P=P, pO=pO):
                    if have_state:
                        e.matmul(pO[:, cs], lhsT=Sb[:], rhs=Qb[:, cs], start=True, stop=False)
                    return e.matmul(pO[:, cs], lhsT=V[:, c, :], rhs=P[:, cs], start=(not have_state), stop=True)
                st.op("tensor", oc, reads=[Qbk, Vk, Pk] + ([Sbk] if have_state else []), writes=[pOk])
                if n == NT - 1 and c == 7:
                    break
                pU, pUk = st.nxt("pU")
                st.op("tensor", lambda e, c=c, KdT=KdT, V=V, pU=pU: e.matmul(pU[:], lhsT=KdT[:, c, :], rhs=V[:, c, :], start=True, stop=True),
                      reads=[KdTk, Vk], writes=[pUk])
                if not have_state:
                    st.op("vector", lambda e, pU=pU: e.tensor_copy(out=Sf[:], in_=pU[:]), reads=[pUk], writes=["Sf"])
                else:
                    st.op("vector", lambda e, c=c, pU=pU, E3=E3: e.scalar_tensor_tensor(
                        out=Sf[:], in0=Sf[:], scalar=E3[:, c * 64 + 63:c * 64 + 64], in1=pU[:], op0=ALU.mult, op1=ALU.add),
                        reads=[pUk, E3k], writes=["Sf"])
                Sb, Sbk = st.nxt("Sb")
                st.op("scalar", lambda e, Sb=Sb: e.copy(out=Sb[:], in_=Sf[:]), reads=["Sf"], writes=[Sbk])
            osq, osqk = st.nxt("osq")
            rs, rsk = st.nxt("rs")
            y, yk = st.nxt("y")
            yb, ybk = st.nxt("yb")
            p2, p2k = st.nxt("p2")
            st.op("scalar", lambda e, osq=osq, pO=pO: e.activation(out=osq[:], in_=pO[:], func=ACT.Square), reads=[pOk], writes=[osqk])
            st.op("tensor", lambda e, p2=p2, osq=osq: e.matmul(p2[:], lhsT=onesf[:], rhs=osq[:], start=True, stop=True),
                  reads=[osqk, "onesf"], writes=[p2k])
            st.op("scalar", lambda e, rs=rs, p2=p2: e.activation(out=rs[:], in_=p2[:], func=ACT.Sqrt, bias=EPS, scale=1.0 / 128.0),
                  reads=[p2k], writes=[rsk])
            st.op("vector", lambda e, rs=rs: e.reciprocal(out=rs[:], in_=rs[:]), reads=[], writes=[rsk])
            st.op("vector", lambda e, y=y, pO=pO, rs=rs: e.tensor_tensor(out=y[:], in0=pO[:], in1=rs[:], op=ALU.mult), reads=[pOk, rsk], writes=[yk])
            st.op("vector", lambda e, y=y, yb=yb, raw=raw, h=h: e.scalar_tensor_tensor(
                out=yb[:], in0=y[:], scalar=hgn[:, h:h + 1], in1=raw[:, 2, :], op0=ALU.mult, op1=ALU.mult),
                reads=[yk, rawk + "2", "hgn"], writes=[ybk])
            st.dma("sync", mixT[orow + h * 128:orow + (h + 1) * 128, t0:t0 + 512], yb[:], reads=[ybk], semkey="st" + ybk)
    st.finish()


def host_consts(hh):
    C = {}
    C["ident"] = np.eye(128, dtype=np.float32)
    pos = np.arange(S, dtype=np.float32)
    invf = (np.float32(10000.0) ** (-np.arange(128, dtype=np.float32) * np.float32(2.0) / np.float32(256))).astype(np.float32)
    ang = (invf[:, None] * pos[None, :]).astype(np.float32)
    C["cosR"] = np.cos(ang).astype(np.float32)
    C["sinR"] = np.sin(ang).astype(np.float32)
    M = np.zeros((2, 128, 512), np.float32)
    qd = np.zeros((2, 128, 512), np.float32)
    kd = np.zeros((2, 128, 4), np.float32)
    g5 = np.zeros((2, 128, 1), np.float32)
    s = np.arange(128)[:, None]
    u = np.arange(512)[None, :]
    for hl in range(2):
        hg = hh * 2 + hl
        lg = np.log1p(-np.exp2(-5.0 - hg))
        M[hl] = np.where(u >= s, np.exp(lg * np.maximum(u - s, 0)), 0.0)
        qd[hl] = np.exp(lg * (np.arange(512) + 1.0))[None, :]
        for jt in range(4):
            kd[hl, :, jt] = np.exp(lg * (511.0 - (jt * 128 + np.arange(128))))
        g5[hl] = np.exp(lg * 512.0)
    C["retM"], C["retqd"], C["retkd"], C["retg5"] = [np.ascontiguousarray(np.swapaxes(a, 0, 1)) for a in (M, qd, kd, g5)]
    invm = (np.float32(500000.0) ** (-np.arange(16, dtype=np.float32) * np.float32(2.0) / np.float32(32))).astype(np.float32)
    angm = (invm[:, None] * pos[None, :]).astype(np.float32)
    cosM = np.ones((128, S), np.float32)
    sinM = np.zeros((128, S), np.float32)
    cosM[0:16] = np.cos(angm)
    cosM[16:32] = np.cos(angm)
    sinM[0:16] = np.sin(angm)
    sinM[16:32] = np.sin(angm)
    C["cosM"], C["sinM"] = cosM, sinM
    R = np.zeros((128, 128), np.float32)
    for d in range(16):
        R[d, d + 16] = -1.0
        R[d + 16, d] = 1.0
    C["RT"] = np.ascontiguousarray(R.T)
    pastb = np.zeros((128, 16, 16), np.float32)
    ownfix = np.full((128, 16, 16), -1e9, np.float32)
    for qb in range(16):
        pastb[:, qb, qb:] = -1e30
        ownfix[:, qb, qb] = 0.0
    C["pastb"] = pastb.reshape(128, 256)
    C["ownfix"] = ownfix.reshape(128, 256)
    E = np.zeros((16, 16, 128), np.float32)
    for b in range(16):
        E[b, b, :] = 1.0
    C["Eoh"] = E.reshape(16, 2048)
    cb = np.zeros((4, 128, 512), np.float32)
    t = np.arange(512)[None, :]
    for j in range(4):
        kp = 128 * j + np.arange(128)[:, None]
        cb[j] = np.where((t // 256 == kp // 256) & (kp > t), NEG, 0.0)
    C["causb"] = cb
    seg = np.ones((128, 512), np.float32)
    seg[:, ::64] = 0.0
    C["segmask"] = seg
    tri = np.zeros((64, 8, 64), np.float32)
    jj = np.arange(64)[:, None]
    ii = np.arange(64)[None, :]
    tri[:, :, :] = np.where(ii >= jj, 1.0, 0.0)[:, None, :]
    C["tril"] = tri.reshape(64, 512)
    return C


CONST_SHAPES = {
    "ident": [128, 128], "cosR": [128, S], "sinR": [128, S], "retM": [128, 2, 512], "retqd": [128, 2, 512],
    "retkd": [128, 2, 4], "retg5": [128, 2, 1], "cosM": [128, S], "sinM": [128, S], "RT": [128, 128],
    "pastb": [128, 256], "ownfix": [128, 256], "Eoh": [16, 2048], "causb": [4, 128, 512],
    "segmask": [128, 512], "tril": [64, 512],
}

RQ, RK, RG, MQ, MK, HQ, HF, HG = [i * 512 for i in range(8)]
TV_R, TV_M, TV_H = 0, 512, 1024


def build_A(layer, do=("ret", "moba", "hgrn")):
    nc = bass.Bass("TRN2", target_bir_lowering=False)
    ins = {}

    def inp(name, shape, dt=F32):
        ins[name] = nc.dram_tensor(name, list(shape), dt, kind="ExternalInput").ap()
        return ins[name]
    xT = inp("xT", [D, S])
    nmix = inp("nmix", [128, 16])
    wfm = inp("wfm", [D, 4096])
    wtm = inp("wtm", [D, 1536])
    retn = inp("retn", [128, 4])
    hgn = inp("hgn", [128, 4])
    hlb = inp("hlb", [128, 8])
    C = {k: inp(k, shp) for k, shp in CONST_SHAPES.items()}
    C["retn"], C["hgn"], C["hlb"] = retn, hgn, hlb
    mixT = nc.dram_tensor("mixT", [1536, S], BF16, kind="ExternalOutput").ap()
    hT = nc.dram_tensor("hT", [D, S], BF16).ap()
    pfm = nc.dram_tensor("pfm", [4096, S], F32).ap()
    ptm = nc.dram_tensor("ptm", [S, 1536], BF16).ap()
    prog = Prog(nc)
    with nc.allow_low_precision("bf16 matmul operands, fp32 accumulation"):
        stage_rmsnorm(prog, xT, nmix, hT, S)
        setup, epi = epi_store_f32(pfm)
        linear_fm(prog, "pfm", S, 512, [(hT, D)], [(0, wfm)], 32, setup, epi, G=4)
        linear_tm(prog, "ptm", S, hT, D, wtm, 1536, ptm)
        if "ret" in do:
            stage_retention(prog, pfm, ptm, C, mixT, RQ, RK, RG, TV_R, 0)
        if "moba" in do:
            stage_moba(prog, pfm, ptm, C, mixT, MQ, MK, TV_M, 512)
        if "hgrn" in do:
            stage_hgrn(prog, pfm, ptm, C, mixT, HQ, HF, HG, TV_H, 1024, layer)
    prog.close()
    return nc


_O = np.cumsum([0] + [1024] * 11 + [2048] * 3)
(O_RQ, O_RK, O_RV, O_RG, O_MQ, O_MK, O_MV, O_HQ, O_HF, O_HI, O_HG, O_GR, O_GM, O_GH) = [int(v) for v in _O[:14]]


def _cols(v, n=16):
    return np.ascontiguousarray(np.asarray(v, np.float32).reshape(n, 128).T)


def prep_A(xb, inp, layer, hh, consts=None):
    w = inp["w_in"][layer]
    sl = slice(hh * 512, (hh + 1) * 512)

    def blk(o):
        return w[:, o:o + 1024][:, sl]
    m = {}
    m["xT"] = np.ascontiguousarray(xb.T)
    m["nmix"] = _cols(inp["norm_mix"][layer])
    m["wfm"] = np.ascontiguousarray(np.concatenate([blk(o) for o in (O_RQ, O_RK, O_RG, O_MQ, O_MK, O_HQ, O_HF, O_HG)], axis=1))
    m["wtm"] = np.ascontiguousarray(np.concatenate([blk(o) for o in (O_RV, O_MV, O_HI)], axis=1))
    m["retn"] = _cols(inp["ret_norm"][layer][sl], 4)
    m["hgn"] = _cols(inp["hgrn_norm"][layer][sl], 4)
    hl = np.zeros((128, 8), np.float32)
    for h in range(4):
        for i in range(DEPTH):
            hl[:, h * 2 + i] = inp["hgrn_lower_bounds"][i, (hh * 4 + h) * 128:(hh * 4 + h + 1) * 128]
    m["hlb"] = hl
    m.update(consts if consts is not None else host_consts(hh))
    return m


TB = S // 2


def epi_gated_merge(mxT):
    def setup(st):
        st.pool("sg", 3, [128, 512], F32)
        st.pool("acc", 2, [128, 512], F32)
        st.pool("mo", 2, [128, 512], BF16)
        return {}

    def epi(st, ctx, c, pss, tok0):
        acc, acck = st.nxt("acc")
        for br in range(3):
            pg, pgk = pss[2 * br]
            pb, pbk = pss[2 * br + 1]
            sg, sgk = st.nxt("sg")
            st.op("scalar", lambda e, sg=sg, pg=pg: e.activation(out=sg[:], in_=pg[:], func=ACT.Sigmoid), reads=[pgk], writes=[sgk])
            if br == 0:
                st.op("vector", lambda e, sg=sg, pb=pb: e.tensor_tensor(out=acc[:], in0=pb[:], in1=sg[:], op=ALU.mult),
                      reads=[pbk, sgk], writes=[acck])
            else:
                st.op("vector", lambda e, sg=sg, pb=pb: e.tensor_tensor(out=sg[:], in0=pb[:], in1=sg[:], op=ALU.mult),
                      reads=[pbk], writes=[sgk])
                if br == 1:
                    st.op("gpsimd", lambda e, sg=sg: e.tensor_tensor(out=acc[:], in0=acc[:], in1=sg[:], op=ALU.add),
                          reads=[sgk], writes=[acck])
                else:
                    mo, mok = st.nxt("mo")
                    st.op("gpsimd", lambda e, sg=sg, mo=mo: e.tensor_tensor(out=mo[:], in0=acc[:], in1=sg[:], op=ALU.add),
                          reads=[sgk, acck], writes=[mok])
                    st.dma("sync", mxT[c * 128:(c + 1) * 128, tok0:tok0 + 512], mo[:], reads=[mok], semkey="st" + mok)
    return setup, epi


def epi_residual(resid, out):
    def setup(st):
        st.pool("xr", 3, [128, 512], F32)
        return {}

    def epi(st, ctx, c, pss, tok0):
        ps, psk = pss[0]
        xr, xrk = st.nxt("xr")
        st.dma("sync", xr[:], resid[c * 128:(c + 1) * 128, tok0:tok0 + 512], writes=[xrk])
        st.op("vector", lambda e: e.tensor_tensor(out=xr[:], in0=ps[:], in1=xr[:], op=ALU.add), reads=[psk], writes=[xrk])
        st.dma("sync", out[c * 128:(c + 1) * 128, tok0:tok0 + 512], xr[:], reads=[xrk], semkey="st" + xrk)
    return setup, epi


def epi_swiglu(aT):
    def setup(st):
        st.pool("sg", 3, [128, 512], F32)
        st.pool("ao", 3, [128, 512], BF16)
        return {}

    def epi(st, ctx, c, pss, tok0):
        pg, pgk = pss[0]
        pu, puk = pss[1]
        sg, sgk = st.nxt("sg")
        ao, aok = st.nxt("ao")
        st.op("scalar", lambda e: e.activation(out=sg[:], in_=pg[:], func=ACT.Silu), reads=[pgk], writes=[sgk])
        st.op("vector", lambda e: e.tensor_tensor(out=ao[:], in0=pu[:], in1=sg[:], op=ALU.mult), reads=[puk, sgk], writes=[aok])
        st.dma("sync", aT[c * 128:(c + 1) * 128, tok0:tok0 + 512], ao[:], reads=[aok], semkey="st" + aok)
    return setup, epi


def build_B(final):
    nc = bass.Bass("TRN2", target_bir_lowering=False)

    def inp(name, shape, dt=F32):
        return nc.dram_tensor(name, list(shape), dt, kind="ExternalInput").ap()
    T = TB
    xT = inp("xT", [D, T])
    mixT = inp("mixT", [3072, T], BF16)
    nmix = inp("nmix", [128, 16])
    wg = inp("wg", [D, 6144])
    wbr = inp("wbr", [3072, D])
    wo = inp("wo", [D, D])
    nffn = inp("nffn", [128, 16])
    wfg = inp("wfg", [D, FF])
    wfu = inp("wfu", [D, FF])
    wfd = inp("wfd", [FF, D])
    fnorm = inp("fnorm", [128, 16])
    xoT = nc.dram_tensor("xoT", [D, T], F32, kind="ExternalOutput").ap()
    hT = nc.dram_tensor("hT", [D, T], BF16).ap()
    mxT = nc.dram_tensor("mxT", [D, T], BF16).ap()
    x1T = nc.dram_tensor("x1T", [D, T], F32).ap()
    h2T = nc.dram_tensor("h2T", [D, T], BF16).ap()
    aT = nc.dram_tensor("aT", [FF, T], BF16).ap()
    x2T = nc.dram_tensor("x2T", [D, T], F32).ap() if final else xoT
    prog = Prog(nc)
    with nc.allow_low_precision("bf16 matmul operands, fp32 accumulation"):
        stage_rmsnorm(prog, xT, nmix, hT, T)
        srcs = [(hT, D)] + [(mixT[i * 1024:(i + 1) * 1024, :], 1024) for i in range(3)]
        jobs = []
        for br in range(3):
            jobs.append((0, wg[:, br * 2048:(br + 1) * 2048]))
            jobs.append((1 + br, wbr[br * 1024:(br + 1) * 1024, :]))
        setup, epi = epi_gated_merge(mxT)
        linear_fm(prog, "gm", T, 512, srcs, jobs, 16, setup, epi, G=2)
        setup, epi = epi_residual(xT, x1T)
        linear_fm(prog, "wo", T, 512, [(mxT, D)], [(0, wo)], 16, setup, epi, G=4)
        stage_rmsnorm(prog, x1T, nffn, h2T, T)
        setup, epi = epi_swiglu(aT)
        linear_fm(prog, "up", T, 512, [(h2T, D)], [(0, wfg), (0, wfu)], 44, setup, epi, G=4)
        setup, epi = epi_residual(x1T, x2T)
        linear_fm(prog, "dn", T, 512, [(aT, FF)], [(0, wfd)], 16, setup, epi, G=2)
        if final:
            stage_rmsnorm(prog, x2T, fnorm, None, T, out_f32=xoT)
    prog.close()
    return nc


def prep_B(xb_tok, mix_tok, inp, layer):
    w = inp["w_in"][layer]
    m = {}
    m["xT"] = np.ascontiguousarray(xb_tok.T)
    m["mixT"] = np.ascontiguousarray(mix_tok)
    m["nmix"] = _cols(inp["norm_mix"][layer])
    m["wg"] = np.ascontiguousarray(w[:, O_GR:O_GR + 6144])
    m["wbr"] = np.ascontiguousarray(np.concatenate([inp["w_branch_ret"][layer], inp["w_branch_moba"][layer],
                                                    inp["w_branch_hgrn"][layer]], axis=0))
    m["wo"] = np.ascontiguousarray(inp["w_out"][layer])
    m["nffn"] = _cols(inp["norm_ffn"][layer])
    m["wfg"] = np.ascontiguousarray(inp["w_ffn_gate"][layer])
    m["wfu"] = np.ascontiguousarray(inp["w_ffn_up"][layer])
    m["wfd"] = np.ascontiguousarray(inp["w_ffn_down"][layer])
    m["fnorm"] = _cols(inp["final_norm"])
    return m


_NC_CACHE = {}


def _get_nc(kind, arg):
    key = (kind, arg)
    if key not in _NC_CACHE:
        _NC_CACHE[key] = build_A(arg) if kind == "A" else build_B(arg)
    return _NC_CACHE[key]


def kernel(**inputs):
    inp = {k: np.asarray(v) for k, v in inputs.items()}
    x = np.asarray(inp["x"], np.float32)
    consts = [host_consts(0), host_consts(1)]
    cores = list(range(8))
    for layer in range(DEPTH):
        ncA = _get_nc("A", layer)
        mapsA = [prep_A(x[c // 2], inp, layer, c % 2, consts[c % 2]) for c in cores]
        resA = run_bass_kernel_spmd(ncA, mapsA, core_ids=cores)
        del mapsA
        mixfull = []
        for b in range(NB_):
            parts = [np.asarray(resA.results[b * 2 + hh]["mixT"]) for hh in range(2)]
            full = np.concatenate([parts[hh][o:o + 512] for o in (0, 512, 1024) for hh in range(2)], axis=0)
            mixfull.append(full)
        ncB = _get_nc("B", layer == DEPTH - 1)
        mapsB = []
        for c in cores:
            b, th = c // 2, c % 2
            tok = slice(th * TB, (th + 1) * TB)
            mapsB.append(prep_B(x[b, tok], mixfull[b][:, tok], inp, layer))
        resB = run_bass_kernel_spmd(ncB, mapsB, core_ids=cores)
        del mapsB
        xn = np.empty_like(x)
        for c in cores:
            b, th = c // 2, c % 2
            xn[b, th * TB:(th + 1) * TB] = np.asarray(resB.results[c]["xoT"]).T
        x = xn
    return x
```
